# Optimizing a Trainium2 kernel written in Bass

```python
import math
import jax, jax.numpy as jnp
from jax import lax
import numpy as np

D_MODEL = 1024
BATCH = 8
SEQ = 4096
DEPTH = 4
DEC_BATCH = 1
DEC_SEQ = 16384
PAST_LEN = 128

GRID_W = 64
HEAD_DIM = 64
N_HEADS_A = 8
NA_ROWS = 8
NA_COLS = 16
N_HEADS_B = 8
N_KV_B = 2
GROUP_B = N_HEADS_B // N_KV_B
ROPE_THETA = 10000.0
C_WIDTH = 512
CONV_K = 31
N_HEADS_D = 4
D_V_DIFF = 2 * HEAD_DIM
Q_BLOCK = 128
D_FF = -(-8 * D_MODEL // (3 * 256)) * 256
EPS = 1e-6
LN_EPS = 1e-5
N_AB = (DEPTH + 1) // 2
N_CD = DEPTH // 2
W_A = N_HEADS_A * HEAD_DIM
W_BQ = N_HEADS_B * HEAD_DIM
W_BKV = N_KV_B * HEAD_DIM
W_DQ = N_HEADS_D * 2 * HEAD_DIM
W_DV = N_HEADS_D * D_V_DIFF
AB_SPLITS = (W_A, W_A, W_A, W_BQ, W_BKV, W_BKV)
CD_SPLITS = (C_WIDTH, C_WIDTH, W_DQ, W_DQ, W_DV)
AB_IN = sum(AB_SPLITS)
CD_IN = sum(CD_SPLITS)
AB_OUT = W_A + W_BQ
CD_OUT = C_WIDTH + W_DV
ALIBI_SLOPES = tuple(2.0 ** (-8.0 * (h + 1) / N_HEADS_D) for h in range(N_HEADS_D))

kernel_name = 'hybrid_natten_gqa_conformer_diffattn_encoder'

F32 = jnp.float32


def _split(z, sizes):
    offs = np.cumsum(sizes)[:-1].tolist()
    return jnp.split(z, offs, axis=-1)


def _rms_norm(x, g):
    xf = x.astype(F32)
    y = xf * lax.rsqrt(jnp.mean(xf * xf, axis=-1, keepdims=True) + EPS)
    return (y * g.astype(F32)).astype(x.dtype)


def _layer_norm(x, g, b):
    xf = x.astype(F32)
    mu = jnp.mean(xf, axis=-1, keepdims=True)
    var = jnp.mean(jnp.square(xf - mu), axis=-1, keepdims=True)
    y = (xf - mu) * lax.rsqrt(var + LN_EPS)
    return (y * g.astype(F32) + b.astype(F32)).astype(x.dtype)


def _axial_rope(x):
    L = x.shape[1]
    t = jnp.arange(L, dtype=jnp.int32)
    row = (t // GRID_W).astype(F32)
    col = (t % GRID_W).astype(F32)
    half = HEAD_DIM // 2
    inv = ROPE_THETA ** (-jnp.arange(0, half, 2, dtype=F32) / half)
    ang = jnp.concatenate([row[:, None] * inv, col[:, None] * inv], axis=-1)
    cos = jnp.cos(ang)[None, :, None, :].astype(x.dtype)
    sin = jnp.sin(ang)[None, :, None, :].astype(x.dtype)
    x1 = x[..., 0::2]
    x2 = x[..., 1::2]
    out = jnp.stack([x1 * cos - x2 * sin, x1 * sin + x2 * cos], axis=-1)
    return out.reshape(x.shape)


def _neighborhood_attention(q, k, v, rpb):
    B, L, H, dh = q.shape
    rows = L // GRID_W
    kh = min(NA_ROWS, rows)
    n_nb = kh * NA_COLS
    scale = dh ** -0.5
    col = np.arange(GRID_W)
    cs = np.clip(col - NA_COLS // 2, 0, GRID_W - NA_COLS)
    kcol = cs[:, None] + np.arange(NA_COLS)[None, :]
    dcol = kcol - col[:, None]
    qb = q.reshape(B, rows, GRID_W, H, dh).swapaxes(0, 1)

    def row_block(args):
        qblk, r = args
        rs = jnp.clip(r - kh // 2, 0, rows - kh)
        krow = rs + jnp.arange(kh, dtype=jnp.int32)
        idx = (krow[None, :, None] * GRID_W + kcol[:, None, :]).reshape(GRID_W, n_nb)
        kn = k[:, idx]
        vn = v[:, idx]
        s = jnp.einsum('bqhd,bqnhd->bhqn', qblk, kn).astype(F32) * scale
        drow = krow - r
        bias = rpb[:, drow[None, :, None] + (NA_ROWS - 1), dcol[:, None, :] + (NA_COLS - 1)]
        s = s + bias.reshape(H, GRID_W, n_nb).astype(F32)[None]
        p = jax.nn.softmax(s, axis=-1).astype(v.dtype)
        return jnp.einsum('bhqn,bqnhd->bqhd', p, vn)

    o = lax.map(row_block, (qb, jnp.arange(rows, dtype=jnp.int32)))
    return o.swapaxes(0, 1).reshape(B, L, H * dh)


def _gqa_attention(q, k, v):
    B, L = q.shape[:2]
    nb = L // Q_BLOCK
    scale = HEAD_DIM ** -0.5
    qb = q.reshape(B, nb, Q_BLOCK, N_KV_B, GROUP_B, HEAD_DIM).swapaxes(0, 1)

    def block(qblk):
        s = jnp.einsum('bqkgd,bskd->bkgqs', qblk, k).astype(F32) * scale
        p = jax.nn.softmax(s, axis=-1).astype(v.dtype)
        return jnp.einsum('bkgqs,bskd->bqkgd', p, v)

    o = lax.map(block, qb)
    return o.swapaxes(0, 1).reshape(B, L, N_HEADS_B * HEAD_DIM)


def _diff_attention(q, k, v, lam):
    B, L = q.shape[:2]
    nb = L // Q_BLOCK
    scale = HEAD_DIM ** -0.5
    slopes = jnp.asarray(ALIBI_SLOPES, F32)
    kpos = jnp.arange(L, dtype=jnp.int32)
    qb = q.reshape(B, nb, Q_BLOCK, 2, N_HEADS_D, HEAD_DIM).swapaxes(0, 1)
    qpos = kpos.reshape(nb, Q_BLOCK)

    def block(args):
        qblk, qp = args
        s = jnp.einsum('bqmhd,bsmhd->bmhqs', qblk, k).astype(F32) * scale
        dist = jnp.abs(qp[:, None] - kpos[None, :]).astype(F32)
        s = s - slopes[:, None, None] * dist
        p = jax.nn.softmax(s, axis=-1)
        a = (p[:, 0] - lam * p[:, 1]).astype(v.dtype)
        return jnp.einsum('bhqs,bshe->bqhe', a, v)

    o = lax.map(block, (qb, qpos))
    return o.swapaxes(0, 1).reshape(B, L, N_HEADS_D, D_V_DIFF)


def _mixer_ab(h, p, j):
    B, L, _ = h.shape
    z = h @ p['w_in_ab'][j]
    qa, ka, va, qb, kb, vb = _split(z, AB_SPLITS)
    shp_a = (B, L, N_HEADS_A, HEAD_DIM)
    oa = _neighborhood_attention(qa.reshape(shp_a), ka.reshape(shp_a), va.reshape(shp_a), p['rpb_a'][j])
    qb = _axial_rope(_rms_norm(qb.reshape(B, L, N_HEADS_B, HEAD_DIM), p['qnorm_b'][j]))
    kb = _axial_rope(_rms_norm(kb.reshape(B, L, N_KV_B, HEAD_DIM), p['knorm_b'][j]))
    vb = vb.reshape(B, L, N_KV_B, HEAD_DIM)
    ob = _gqa_attention(qb, kb, vb)
    return jnp.concatenate([oa, ob], axis=-1) @ p['w_out_ab'][j]


def _mixer_cd(h, p, j, layer_idx):
    B, L, _ = h.shape
    z = h @ p['w_in_cd'][j]
    ca, cg, q, k, v = _split(z, CD_SPLITS)
    u = ca * jax.nn.sigmoid(cg)
    u = lax.conv_general_dilated(u, p['conv_w_c'][j][:, None, :], window_strides=(1,),
                                 padding=[(CONV_K // 2, CONV_K // 2)],
                                 dimension_numbers=('NWC', 'WIO', 'NWC'),
                                 feature_group_count=C_WIDTH)
    u = u + p['conv_b_c'][j]
    u = jax.nn.silu(_layer_norm(u, p['conv_ln_g'][j], p['conv_ln_b'][j]))
    lam_init = 0.8 - 0.6 * math.exp(-0.3 * layer_idx)
    lam = (jnp.exp(jnp.sum(p['lam_q1'][j].astype(F32) * p['lam_k1'][j].astype(F32)))
           - jnp.exp(jnp.sum(p['lam_q2'][j].astype(F32) * p['lam_k2'][j].astype(F32))) + lam_init)
    q = q.reshape(B, L, N_HEADS_D, 2, HEAD_DIM).swapaxes(2, 3)
    k = k.reshape(B, L, N_HEADS_D, 2, HEAD_DIM).swapaxes(2, 3)
    v = v.reshape(B, L, N_HEADS_D, D_V_DIFF)
    od = _diff_attention(q, k, v, lam)
    od = (_rms_norm(od, p['subln_g'][j]) * (1.0 - lam_init)).reshape(B, L, W_DV)
    return jnp.concatenate([u, od], axis=-1) @ p['w_out_cd'][j]


def _trunk(x, c, p):
    cs = jax.nn.silu(c)
    for li in range(DEPTH):
        mod = cs @ p['w_mod'][li] + p['b_mod'][li]
        sh1, sc1, g1, sh2, sc2, g2 = [m[:, None, :] for m in jnp.split(mod, 6, axis=-1)]
        h = _rms_norm(x, p['norm_mix_g'][li]) * (1.0 + sc1) + sh1
        if li % 2 == 0:
            out = _mixer_ab(h, p, li // 2)
        else:
            out = _mixer_cd(h, p, li // 2, li)
        x = x + (1.0 + g1) * out
        h = _rms_norm(x, p['norm_ffn_g'][li]) * (1.0 + sc2) + sh2
        ff = (jax.nn.silu(h @ p['w1'][li]) * (h @ p['w3'][li])) @ p['w2'][li]
        x = x + (1.0 + g2) * ff
    return _rms_norm(x, p['final_g'])


def setup_inputs(seed: int = 0) -> dict:
    key = jax.random.key(seed)
    ks = jax.random.split(key, 32)

    def nrm(k, shape, s):
        return jax.random.normal(k, shape, F32) * s

    D = D_MODEL
    return {
        'x_prompt': nrm(ks[0], (BATCH, SEQ, D), 1.0),
        'x_sample': nrm(ks[1], (DEC_BATCH, DEC_SEQ, D), 1.0),
        'c_prompt': nrm(ks[2], (BATCH, D), 1.0),
        'c_sample': nrm(ks[3], (DEC_BATCH, D), 1.0),
        'w_mod': nrm(ks[4], (DEPTH, D, 6 * D), 0.1 * D ** -0.5),
        'b_mod': nrm(ks[5], (DEPTH, 6 * D), 0.02),
        'norm_mix_g': 1.0 + nrm(ks[6], (DEPTH, D), 0.05),
        'norm_ffn_g': 1.0 + nrm(ks[7], (DEPTH, D), 0.05),
        'w_in_ab': nrm(ks[8], (N_AB, D, AB_IN), D ** -0.5),
        'rpb_a': nrm(ks[9], (N_AB, N_HEADS_A, 2 * NA_ROWS - 1, 2 * NA_COLS - 1), 0.1),
        'qnorm_b': 1.0 + nrm(ks[10], (N_AB, HEAD_DIM), 0.05),
        'knorm_b': 1.0 + nrm(ks[11], (N_AB, HEAD_DIM), 0.05),
        'w_out_ab': nrm(ks[12], (N_AB, AB_OUT, D), AB_OUT ** -0.5),
        'w_in_cd': nrm(ks[13], (N_CD, D, CD_IN), D ** -0.5),
        'conv_w_c': nrm(ks[14], (N_CD, CONV_K, C_WIDTH), CONV_K ** -0.5),
        'conv_b_c': nrm(ks[15], (N_CD, C_WIDTH), 0.02),
        'conv_ln_g': 1.0 + nrm(ks[16], (N_CD, C_WIDTH), 0.05),
        'conv_ln_b': nrm(ks[17], (N_CD, C_WIDTH), 0.02),
        'lam_q1': nrm(ks[18], (N_CD, HEAD_DIM), 0.1),
        'lam_k1': nrm(ks[19], (N_CD, HEAD_DIM), 0.1),
        'lam_q2': nrm(ks[20], (N_CD, HEAD_DIM), 0.1),
        'lam_k2': nrm(ks[21], (N_CD, HEAD_DIM), 0.1),
        'subln_g': 1.0 + nrm(ks[22], (N_CD, D_V_DIFF), 0.05),
        'w_out_cd': nrm(ks[23], (N_CD, CD_OUT, D), CD_OUT ** -0.5),
        'w1': nrm(ks[24], (DEPTH, D, D_FF), D ** -0.5),
        'w3': nrm(ks[25], (DEPTH, D, D_FF), D ** -0.5),
        'w2': nrm(ks[26], (DEPTH, D_FF, D), D_FF ** -0.5),
        'final_g': 1.0 + nrm(ks[27], (D,), 0.05),
    }


def reference(x_prompt, x_sample, c_prompt, c_sample, w_mod, b_mod, norm_mix_g, norm_ffn_g,
              w_in_ab, rpb_a, qnorm_b, knorm_b, w_out_ab, w_in_cd, conv_w_c, conv_b_c,
              conv_ln_g, conv_ln_b, lam_q1, lam_k1, lam_q2, lam_k2, subln_g, w_out_cd,
              w1, w3, w2, final_g):
    params = dict(w_mod=w_mod, b_mod=b_mod, norm_mix_g=norm_mix_g, norm_ffn_g=norm_ffn_g,
                  w_in_ab=w_in_ab, rpb_a=rpb_a, qnorm_b=qnorm_b, knorm_b=knorm_b, w_out_ab=w_out_ab,
                  w_in_cd=w_in_cd, conv_w_c=conv_w_c, conv_b_c=conv_b_c, conv_ln_g=conv_ln_g,
                  conv_ln_b=conv_ln_b, lam_q1=lam_q1, lam_k1=lam_k1, lam_q2=lam_q2, lam_k2=lam_k2,
                  subln_g=subln_g, w_out_cd=w_out_cd, w1=w1, w3=w3, w2=w2, final_g=final_g)
    y_prompt = _trunk(x_prompt, c_prompt, params)
    y_sample = _trunk(x_sample, c_sample, params)
    return (y_prompt, y_sample)
```

```python
import contextlib
import math
import numpy as np
import concourse.bass as bass
import concourse.mybir as mybir
from concourse.bass_utils import run_bass_kernel_spmd

F32 = mybir.dt.float32
BF16 = mybir.dt.bfloat16
U8 = mybir.dt.uint8
I32 = mybir.dt.int32
AF = mybir.ActivationFunctionType
ALU = mybir.AluOpType
ENGS = ("tensor", "vector", "scalar", "gpsimd", "sync")

D = 1024
KC = 8
DFF = 2816
FC = 22
NEG = -30000.0
EPS = 1e-6
LN_EPS = 1e-5
SLOPES = [2.0 ** (-8.0 * (h + 1) / 4) for h in range(4)]


class Buf:
    __slots__ = ("name", "w", "r")

    def __init__(self, name):
        self.name = name
        self.w = {}
        self.r = {}


class T:
    __slots__ = ("ap", "buf")

    def __init__(self, ap, buf):
        self.ap = ap
        self.buf = buf

    def __getitem__(self, idx):
        return T(self.ap[idx], self.buf)


def _merge(d, k, v):
    if d.get(k, 0) < v:
        d[k] = v


class Prog:
    def __init__(self, nc):
        self.nc = nc
        self.ops = {e: [] for e in ENGS}
        self.cnt = {e: 0 for e in ENGS}
        self.seen = {e: {} for e in ENGS}
        self.dma_cnt = {}
        self.sems = {}
        self.NPHYS = 56
        self.phys_of = {}
        self.free = list(range(self.NPHYS))
        self.nbuf = 0
        self.nops = 0

    def buf(self, name=None):
        self.nbuf += 1
        return Buf(f"{name or 'b'}_{self.nbuf}")

    def _waits(self, eng, evs):
        need = {}
        for k, v in evs.items():
            if k == ("eng", eng) and eng == "tensor":
                continue
            if self.seen[eng].get(k, 0) >= v:
                continue
            need[k] = v
        for k, v in need.items():
            self.seen[eng][k] = v
        return list(need.items())

    def _deps(self, reads, writes, part=False):
        evs = {}
        for b in reads:
            for k, v in b.w.items():
                _merge(evs, k, v)
        for b in writes:
            if not part:
                for k, v in b.w.items():
                    _merge(evs, k, v)
            for k, v in b.r.items():
                _merge(evs, k, v)
        return evs

    def _post(self, key, val, reads, writes, part):
        for b in writes:
            if not part:
                b.w = {}
                b.r = {}
            _merge(b.w, key, val)
        for b in reads:
            if b not in writes:
                _merge(b.r, key, val)

    def op(self, eng, fn, reads=(), writes=(), sig=True, part=False):
        reads = [t.buf for t in reads]
        writes = [t.buf for t in writes]
        waits = self._waits(eng, self._deps(reads, writes, part))
        key = ("eng", eng)
        if sig:
            self.cnt[eng] += 1
            val = self.cnt[eng]
        else:
            val = self.cnt[eng] + 1
        self.ops[eng].append((waits, fn, key if sig else None))
        self._post(key, val, reads, writes, part)
        self.nops += 1

    def dma(self, eng, fn, reads=(), writes=(), sem=None, part=False):
        reads = [t.buf for t in reads]
        writes = [t.buf for t in writes]
        waits = self._waits(eng, self._deps(reads, writes, part))
        nm = sem.buf.name
        if nm not in self.phys_of:
            self.phys_of[nm] = self.free.pop(0)
        key = ("dma", self.phys_of[nm])
        self.dma_cnt[key] = self.dma_cnt.get(key, 0) + 1
        val = 16 * self.dma_cnt[key]
        self.ops[eng].append((waits, fn, key))
        self._post(key, val, reads, writes, part)
        self.nops += 1

    def cc(self, fn, reads=(), writes=()):
        reads = [t.buf for t in reads]
        writes = [t.buf for t in writes]
        waits = self._waits("gpsimd", self._deps(reads, writes, True))
        self.ncc = getattr(self, "ncc", 0) + 1
        key = ("cc", self.ncc)
        self.cc_keys = getattr(self, "cc_keys", []) + [key]
        self.ops["gpsimd"].append((waits, fn, key))
        self._post(key, 1, reads, writes, True)

    def barrier(self):
        evs = {("eng", e): self.cnt[e] for e in ENGS if self.cnt[e] > 0}
        for k, v in self.dma_cnt.items():
            evs[k] = 16 * v
        for k in getattr(self, "cc_keys", []):
            evs[k] = 1
        for e in ENGS:
            waits = self._waits(e, evs)
            if waits:
                self.ops[e].append((waits, None, None))
        self.phys_of = {}
        self.free = list(range(self.NPHYS))

    def emit(self):
        nc = self.nc
        keys = [("eng", e) for e in ENGS if self.cnt[e] > 0] + list(self.dma_cnt.keys()) + list(getattr(self, "cc_keys", []))
        with contextlib.ExitStack() as st:
            for i, k in enumerate(keys):
                self.sems[k] = st.enter_context(nc.semaphore(f"s{i}"))
            block = st.enter_context(nc.Block())
            for e in ENGS:
                lst = self.ops[e]
                if not lst:
                    continue

                def body(engh, lst=lst):
                    for (waits, fn, inckey) in lst:
                        for (k, v) in waits:
                            engh.wait_ge(self.sems[k], v)
                        if fn is None:
                            continue
                        ins = fn(engh)
                        if inckey is not None:
                            ins.then_inc(self.sems[inckey], 16 if inckey[0] == "dma" else 1)

                getattr(block, e)(body)


class Cfg:
    def __init__(self, Lp=4096, Ls=2048, depth=4, ncores=8):
        self.Lp, self.Ls, self.depth, self.ncores = Lp, Ls, depth, ncores
        self.NTOK = Lp + Ls
        self.rows_p = Lp // 64
        self.rows_s = Ls // 64
        self.NTp = Lp // 512
        self.NTs = Ls // 512
        self.NT = self.NTp + self.NTs
        self.NAB = (depth + 1) // 2
        self.NCD = depth // 2
        self.Lmax = max(Lp, Ls)
        self.TF = 256


def pp_layout(cfg):
    off = {}
    n = 0

    def add(name, w):
        nonlocal n
        off[name] = (n, w)
        n += w
    for li in range(cfg.depth):
        add(f"gmix{li}", 8)
        add(f"gffn{li}", 8)
        add(f"bmod{li}", 48)
    add("gfin", 8)
    for j in range(cfg.NAB):
        for nm in ("qn", "qns", "kn", "kns"):
            add(f"{nm}{j}", 1)
    for j in range(cfg.NCD):
        add(f"convw{j}", 4 * 31)
        add(f"convb{j}", 4)
        add(f"lng{j}", 4)
        add(f"lnb{j}", 4)
        add(f"subg{j}", 2)
        for nm in ("lq1", "lk1", "lq2", "lk2"):
            add(f"{nm}{j}", 64)
    return off, n


class Builder:
    def __init__(self, cfg):
        self.cfg = cfg
        self.nc = bass.Bass("TRN2", target_bir_lowering=False)
        self.p = Prog(self.nc)
        self.st = contextlib.ExitStack()
        self.ARENA = 207 * 1024
        self.persist_top = 0
        self.top = 0

    def sb(self, shape, dt, persist=False, name=None):
        esz = 4 if dt in (F32, I32) else 2
        n = int(np.prod(shape)) * esz
        n = (n + 63) // 64 * 64
        off = self.top
        self.top += n
        assert self.top <= self.ARENA, f"SBUF arena overflow {self.top}"
        if persist:
            assert off == self.persist_top
            self.persist_top = self.top
        ap = self.arena[:, off:off + int(np.prod(shape)) * esz].bitcast(dt)
        if len(shape) > 1:
            names = [f"d{i}" for i in range(len(shape))]
            ap = ap.rearrange(f"p ({' '.join(names)}) -> p {' '.join(names)}", **{n: int(v) for n, v in zip(names[:-1], shape[:-1])})
        return T(ap, self.p.buf(name if (name or '').startswith('sb') else 'sb_' + (name or 't')))

    def sbc(self, nch, TT, dt, name):
        whole = self.sb([nch, TT], dt, name=name)
        chunks = [T(whole.ap[:, k, :], self.p.buf(f"{name}_c{k}")) for k in range(nch)]
        return whole, chunks

    def stage_reset(self):
        self.p.barrier()
        self.top = self.persist_top

    def dram(self, name, shape, dt, kind=None):
        if kind is None:
            t = self.nc.dram_tensor(name, list(shape), dt)
        else:
            t = self.nc.dram_tensor(name, list(shape), dt, kind=kind)
        return T(t.ap(), self.p.buf(name))

    def mm(self, out, lhsT, rhs, start=True, stop=True, sig=None):
        if sig is None:
            sig = True
        self.p.op("tensor", lambda e: e.matmul(out.ap, lhsT=lhsT.ap, rhs=rhs.ap, start=start, stop=stop),
                  reads=[lhsT, rhs], writes=[out], sig=sig, part=True if not start else False)

    def tr(self, out, in_, ident, part=True):
        self.p.op("tensor", lambda e: e.transpose(out.ap, in_.ap, ident.ap), reads=[in_, ident], writes=[out], part=part)

    def act(self, out, in_, func, scale=1.0, bias=None, eng="scalar", part=False):
        reads = [in_]
        kw = {}
        if isinstance(scale, T):
            reads.append(scale)
            kw["scale"] = scale.ap
        else:
            kw["scale"] = scale
        if isinstance(bias, T):
            reads.append(bias)
            kw["bias"] = bias.ap
        elif bias is not None:
            kw["bias"] = bias
        self.p.op(eng, lambda e: e.activation(out=out.ap, in_=in_.ap, func=func, **kw), reads=reads, writes=[out], part=part)

    def tt(self, out, a, b, op, eng="vector", part=False):
        self.p.op(eng, lambda e: e.tensor_tensor(out=out.ap, in0=a.ap, in1=b.ap, op=op), reads=[a, b], writes=[out], part=part)

    def ts(self, out, a, s1, op0, s2=None, op1=None, eng="vector", part=False):
        reads = [a]
        v1 = s1.ap if isinstance(s1, T) else s1
        v2 = s2.ap if isinstance(s2, T) else s2
        if isinstance(s1, T):
            reads.append(s1)
        if isinstance(s2, T):
            reads.append(s2)
        if op1 is None:
            self.p.op(eng, lambda e: e.tensor_scalar(out=out.ap, in0=a.ap, scalar1=v1, scalar2=None, op0=op0), reads=reads, writes=[out], part=part)
        else:
            self.p.op(eng, lambda e: e.tensor_scalar(out=out.ap, in0=a.ap, scalar1=v1, scalar2=v2, op0=op0, op1=op1), reads=reads, writes=[out], part=part)

    def stt(self, out, a, s, b, op0, op1, part=False):
        reads = [a, b]
        v = s.ap if isinstance(s, T) else s
        if isinstance(s, T):
            reads.append(s)
        self.p.op("vector", lambda e: e.scalar_tensor_tensor(out=out.ap, in0=a.ap, scalar=v, in1=b.ap, op0=op0, op1=op1), reads=reads, writes=[out], part=part)

    def copy(self, out, in_, eng="vector", part=False):
        self.p.op(eng, lambda e: e.tensor_copy(out=out.ap, in_=in_.ap), reads=[in_], writes=[out], part=part)

    def memset(self, out, val, eng="vector", part=False):
        self.p.op(eng, lambda e: e.memset(out.ap, val), writes=[out], part=part)

    def recip(self, out, in_, part=False):
        self.p.op("vector", lambda e: e.reciprocal(out=out.ap, in_=in_.ap), reads=[in_], writes=[out], part=part)

    def dma(self, out, in_, eng="sync", part=False, sem=None, xr=(), xw=(), **kw):
        if sem is None:
            sem = out if out.buf.name.startswith("sb") or not in_.buf.name.startswith("sb") else in_
        self.p.dma(eng, lambda e: e.dma_start(out=out.ap, in_=in_.ap, **kw), reads=[in_] + list(xr), writes=[out] + list(xw), sem=sem, part=part)

    def dyn_dma(self, out, src_fn, src_t, reg_name, part=False):
        def f(e):
            src = src_fn(self.vals[reg_name])
            try:
                return e.dma_start(out=out.ap, in_=src)
            except Exception:
                import traceback
                traceback.print_exc()
                print("DYN_DMA FAIL", reg_name)
                raise
        self.p.dma("sync", f, reads=[src_t], writes=[out], sem=out, part=part)

    def build(self):
        cfg = self.cfg
        nc = self.nc
        st = self.st
        NTOK = cfg.NTOK
        ppoff, NPP = pp_layout(cfg)
        self.ppoff = ppoff
        ext = lambda n, s, dt=F32: self.dram(n, s, dt, kind="ExternalInput")
        self.xin = ext("xin", [NTOK, D])
        self.cin = ext("cin", [2, D])
        self.ppvec_d = ext("ppvec", [128, NPP])
        self.ident_d = ext("ident", [128, 128])
        self.bd_d = ext("bd", [128, 128])
        self.e16_d = ext("e16", [16, 1024])
        self.rope_d = ext("rope", [4, 128, NTOK])
        self.mq_d = ext("mq", [16, cfg.NT * 512])
        self.kaugl_d = ext("kaugl", [4, cfg.Lmax])
        self.qaug_d = ext("qaug", [4, cfg.Lmax * 8])
        self.kaugr_d = ext("kaugr", [4, 8 * cfg.Ls])
        self.dp_d = ext("dpt", [4, 4, 128, 512])
        self.info_d = ext("info", [1, 8], I32)
        self.w_mod = ext("w_mod", [cfg.depth, D, 6 * D])
        self.w_in_ab = ext("w_in_ab", [cfg.NAB, D, 2944])
        self.bt_d = ext("bt", [cfg.NAB, 8, 8, 128, 512])
        self.w_out_ab = ext("w_out_ab", [cfg.NAB, D, D])
        self.w_in_cd = ext("w_in_cd", [max(cfg.NCD, 1), D, 2560])
        self.w_out_cd = ext("w_out_cd", [max(cfg.NCD, 1), D, D])
        self.w1 = ext("w1", [cfg.depth, D, DFF])
        self.w3 = ext("w3", [cfg.depth, D, DFF])
        self.w2 = ext("w2", [cfg.depth, DFF, D])
        self.yout = self.dram("yout", [NTOK, D], F32, kind="ExternalOutput")
        Ls = cfg.Ls
        self.xT = self.dram("xT", [8, 128, NTOK], F32)
        self.xT_tiles = {}
        self.aoT = self.dram("aoT", [8, 128, NTOK], BF16)
        self.qaT = self.dram("qaT", [4, 128, NTOK], BF16)
        self.kaT = self.dram("kaT", [4, 128, NTOK], BF16)
        self.va = self.dram("va", [NTOK, 512], BF16)
        self.qbT = self.dram("qbT", [4, 128, NTOK], BF16)
        self.kbT = self.dram("kbT", [128, NTOK], BF16)
        self.vb = self.dram("vb", [NTOK, 128], BF16)
        self.uT = self.dram("uT", [4, 128, NTOK], BF16)
        self.qdT = self.dram("qdT", [4, 128, NTOK], BF16)
        self.kdT = self.dram("kdT", [4, 128, NTOK], BF16)
        self.vd = self.dram("vd", [NTOK, 512], BF16)
        self.l_kb = self.dram("l_kb", [128, Ls], BF16)
        self.g_kb = self.dram("g_kb", [10 * 128, Ls], BF16)
        self.l_vb = self.dram("l_vb", [Ls, 128], BF16)
        self.g_vb = self.dram("g_vb", [10 * Ls, 128], BF16)
        self.l_kah = self.dram("l_kah", [1024, 256], BF16)
        self.g_kah = self.dram("g_kah", [10 * 1024, 256], BF16)
        self.l_vah = self.dram("l_vah", [512, 512], BF16)
        self.g_vah = self.dram("g_vah", [10 * 512, 512], BF16)
        self.l_kd = self.dram("l_kd", [512, Ls], BF16)
        self.g_kd = self.dram("g_kd", [10 * 512, Ls], BF16)
        self.l_vd = self.dram("l_vd", [Ls, 512], BF16)
        self.g_vd = self.dram("g_vd", [10 * Ls, 512], BF16)
        self.l_uh = self.dram("l_uh", [1024, 16], BF16)
        self.g_uh = self.dram("g_uh", [10 * 1024, 16], BF16)

        self.arena = st.enter_context(nc.sbuf_tensor("arena", [128, self.ARENA], U8))
        self.ps = []
        for i in range(8):
            t = st.enter_context(nc.psum_tensor(f"ps{i}", [128, 512], F32))
            self.ps.append(T(t[:, :], self.p.buf(f"ps{i}")))
        self.regs = {}
        self.reg_max = {"L1024": 9 * 1024, "R1024": 9 * 1024, "L512": 9 * 512, "R512": 9 * 512}
        for i, nm in enumerate(("L1024", "R1024", "L512", "R512")):
            self.regs[nm] = st.enter_context(nc.sync.register(nm))

        def ldregs(e):
            ins = None
            for i, nm in enumerate(("L1024", "R1024", "L512", "R512")):
                ins = e.reg_load(self.regs[nm], self.info_d.ap[0:1, i:i + 1])
            self.vals = {nm: e.snap(self.regs[nm], min_val=0, max_val=self.reg_max[nm]) for nm in self.regs}
            return e.nop()
        self.p.op("sync", ldregs, reads=[self.info_d], sig=True)

        import os
        nstop = int(os.environ.get("KSTOP", "999"))
        stages = [self.stage_init]
        for li in range(cfg.depth):
            if li % 2 == 0:
                stages += [lambda li=li: self.stage_A_ab(li), lambda li=li: self.stage_B_ab(li)]
            else:
                stages += [lambda li=li: self.stage_A_cd(li), lambda li=li: self.stage_B_cd(li)]
            stages += [lambda li=li: self.stage_CD(li)]
        for i, f in enumerate(stages):
            if i < nstop:
                f()
        self.stage_final()
        self.p.barrier()
        self.p.emit()
        self.st.close()
        return nc

    def pp(self, name, col=0, w=1, rows=slice(0, 128)):
        o, _ = self.ppoff[name]
        return self.ppv[rows, o + col:o + col + w]

    def seq_of_tile(self, t):
        return 0 if t < self.cfg.NTp else 1

    def stage_init(self):
        cfg = self.cfg
        g = self
        self.ppv = g.sb([pp_layout(cfg)[1]], F32, persist=True, name="sb_ppv")
        self.ident = g.sb([128], F32, persist=True, name="sb_ident")
        self.onesf = g.sb([128], F32, persist=True, name="sb_ones")
        self.bd = g.sb([128], F32, persist=True, name="sb_bd")
        self.e16 = g.sb([1024], BF16, persist=True, name="sb_e16")
        self.modT = g.sb([cfg.depth, 48, 2], F32, persist=True, name="sb_modT")
        self.modA = g.sb([cfg.depth, 2, 8, 2], F32, persist=True, name="sb_modA")
        self.modG = g.sb([cfg.depth, 2, 8, 2], F32, persist=True, name="sb_modG")
        self.lam = g.sb([max(cfg.NCD, 1), 4], F32, persist=True, name="sb_lam")
        self.subg = g.sb([max(cfg.NCD, 1), 2], F32, persist=True, name="sb_subg")
        self.zeros_bf = g.sb([512], BF16, persist=True, name="sb_zeros")
        self.fence_t = [g.sb([32], BF16, persist=True, name=f"sb_fence{i}") for i in range(2)]
        g.dma(self.ppv, self.ppvec_d)
        g.dma(self.ident, self.ident_d)
        g.dma(self.bd, self.bd_d)
        g.dma(self.e16[0:16], self.e16_d, eng="gpsimd")
        g.memset(self.onesf, 1.0)
        g.memset(self.zeros_bf, 0.0)
        for gt, rows_per_slot, cols in ((self.g_kah, 1024, 256), (self.g_vah, 512, 512), (self.g_uh, 1024, 16)):
            for s in (0, 9):
                for r0 in range(0, rows_per_slot, 128):
                    dst = gt[s * rows_per_slot + r0: s * rows_per_slot + r0 + 128, :]
                    g.dma(dst, self.zeros_bf[:, 0:cols], part=True, sem=self.zeros_bf)
        csT = g.sb([8, 2], F32, name="sb_csT")
        for s_ in range(2):
            g.dma(csT[:, :, s_], T(self.cin.ap[s_].rearrange("(k p) -> p k", p=128), self.cin.buf), part=True, allow_slow_non_contiguous=True)
        g.act(csT, csT, AF.Silu)
        wm = [g.sb([8, 512], F32, name=f"sb_wm{i}") for i in range(3)]
        modrow = g.sb([6 * D], F32, name="sb_modrow")
        it = 0
        for li in range(cfg.depth):
            for cg in range(12):
                w = wm[it % 3]
                it += 1
                g.dma(w, T(self.w_mod.ap[li, :, cg * 512:(cg + 1) * 512].rearrange("(k p) c -> p k c", p=128), self.w_mod.buf), eng="sync")
                pst = self.ps[it % 4]
                for k in range(8):
                    g.mm(pst[0:2, :], csT[:, k, :], w[:, k, :], start=(k == 0), stop=(k == 7))
                g.copy(modrow[0:2, cg * 512:(cg + 1) * 512], pst[0:2, :], part=True)
            pt = self.ps[4 + li % 2]
            for j in range(48):
                g.tr(pt[:, 2 * j:2 * j + 2], modrow[0:2, j * 128:(j + 1) * 128], self.ident[0:2, 0:2])
            g.copy(self.modT[:, li], T(pt.ap[:, 0:96].rearrange("p (j s) -> p j s", s=2), pt.buf))
            for s in range(2):
                g.tt(self.modT[:, li, :, s], self.modT[:, li, :, s], self.pp(f"bmod{li}", 0, 48), ALU.add, part=True)
            for half, (gn, sc_i, gate_i) in enumerate(((f"gmix{li}", 1, 2), (f"gffn{li}", 4, 5))):
                for s in range(2):
                    g.stt(self.modA[:, li, half, :, s], self.modT[:, li, sc_i * 8:sc_i * 8 + 8, s], 1.0,
                          self.pp(gn, 0, 8), ALU.add, ALU.mult, part=True)
                g.ts(self.modG[:, li, half], self.modT[:, li, gate_i * 8:gate_i * 8 + 8, :], 1.0, ALU.add, part=True)
        for j in range(cfg.NCD):
            li = 2 * j + 1
            lam_init = 0.8 - 0.6 * math.exp(-0.3 * li)
            tmp = g.sb([64], F32, name="sb_lamtmp")
            for i, (a, b) in enumerate((("lq1", "lk1"), ("lq2", "lk2"))):
                g.tt(tmp, self.pp(f"{a}{j}", 0, 64), self.pp(f"{b}{j}", 0, 64), ALU.mult)
                g.p.op("vector", lambda e, i=i, j=j, tmp=tmp: e.tensor_reduce(out=self.lam.ap[:, j, 2 + i:3 + i], in_=tmp.ap, axis=mybir.AxisListType.X, op=ALU.add),
                       reads=[tmp], writes=[self.lam], part=True)
            g.act(self.lam[:, j, 2:4], self.lam[:, j, 2:4], AF.Exp, part=True)
            g.tt(self.lam[:, j, 0:1], self.lam[:, j, 2:3], self.lam[:, j, 3:4], ALU.subtract, part=True)
            g.ts(self.lam[:, j, 0:1], self.lam[:, j, 0:1], lam_init, ALU.add, part=True)
            g.ts(self.lam[:, j, 1:2], self.lam[:, j, 0:1], -1.0, ALU.mult, part=True)
            g.ts(self.subg[:, j], self.pp(f"subg{j}", 0, 2), 1.0 - lam_init, ALU.mult, part=True)
        xin_t = [g.sb([D], F32, name=f"sb_xin{i}") for i in range(2)]
        xo = [g.sb([8, 512], F32, name=f"sb_xo{i}") for i in range(2)]
        for t in range(cfg.NT):
            o = xo[t % 2]
            for b4 in range(4):
                xi = xin_t[(t * 4 + b4) % 2]
                tok = t * 512 + b4 * 128
                g.dma(xi, self.xin[tok:tok + 128, :])
                for kc in range(8):
                    g.tr(self.ps[kc][:, b4 * 128:(b4 + 1) * 128], xi[:, kc * 128:(kc + 1) * 128], self.ident)
            for kc in range(8):
                if kc % 2 == 0:
                    g.copy(o[:, kc, :], self.ps[kc], part=True)
                else:
                    g.act(o[:, kc, :], self.ps[kc], AF.Copy, part=True)
            g.dma(self.xt_dram(t, 512), o)
        g.stage_reset()

    def xt_dram(self, t, TT):
        key = (t * TT) // 512
        if key not in self.xT_tiles:
            self.xT_tiles[key] = self.p.buf(f"xTt{key}")
        return T(self.xT.ap[:, :, t * TT:(t + 1) * TT].rearrange("k p t -> p k t"), self.xT_tiles[key])

    def load_weight(self, dst, src_ap, srcbuf, nsplit):
        for k in range(nsplit):
            self.dma(dst[:, k, :], T(src_ap[k * 128:(k + 1) * 128, :], srcbuf), eng="gpsimd", part=True)

    def norm_tile(self, xt, A, S, hT, TT, sq, stat, r, tmp):
        g = self
        for kc in range(8):
            g.act(sq[kc % 2], xt[kc], AF.Square)
            g.mm(stat[:, 0:TT], self.onesf, sq[kc % 2], start=(kc == 0), stop=(kc == 7))
        g.act(r, stat[:, 0:TT], AF.Ln, scale=1.0 / D, bias=EPS)
        g.act(r, r, AF.Exp, scale=-0.5)
        for kc in range(8):
            g.stt(tmp[kc % 2], xt[kc], A[:, kc:kc + 1], r, ALU.mult, ALU.mult)
            if S is None:
                g.copy(hT[kc], tmp[kc % 2])
            else:
                g.act(hT[kc], tmp[kc % 2], AF.Identity, bias=S[:, kc:kc + 1])

    def fin_attn(self, acc, dsts, rz, o32):
        g = self
        g.recip(rz[64:65, :], acc[64:65, :])
        bc = self.ps[7]
        g.mm(bc[0:64, :], self.onesf[64:65, 0:64], rz[64:65, :])
        g.act(o32[0:64, :], acc[0:64, :], AF.Copy)
        g.tt(dsts[0], o32[0:64, :], bc[0:64, :], ALU.mult)

    def stage_A_ab(self, li):
        cfg = self.cfg
        g = self
        j = li // 2
        NTOK = cfg.NTOK
        W = g.sb([8, 2944], BF16, name="sb_Wab")
        g.load_weight(W, self.w_in_ab.ap[j], self.w_in_ab.buf, 8)
        xt2 = [g.sbc(8, 512, F32, f"sb_xt{i}") for i in range(2)]
        hT = [g.sb([512], BF16, name=f"sb_hT{k}") for k in range(8)]
        sq = [g.sb([512], F32, name=f"sb_sq{i}") for i in range(2)]
        tmp = [g.sb([512], F32, name=f"sb_tmp{i}") for i in range(2)]
        r = g.sb([512], F32, name="sb_r")
        stg = [g.sb([512], BF16, name=f"sb_stg{i}") for i in range(6)]
        ropet = [g.sb([4, 512], F32, name=f"sb_rope{i}") for i in range(2)]
        t1 = g.sb([512], F32, name="sb_t1")
        t2 = g.sb([512], F32, name="sb_t2")
        r2 = g.sb([512], F32, name="sb_r2")
        sq2 = g.sb([512], F32, name="sb_sq2")
        stat = self.ps[7]
        stat2 = self.ps[6]
        order = list(range(cfg.NTp, cfg.NT)) + list(range(cfg.NTp))
        si = [0]
        pi = [0]

        def nstg():
            si[0] += 1
            return stg[si[0] % 6]

        def nps():
            pi[0] += 1
            return self.ps[pi[0] % 6]

        def loads(n):
            t = order[n]
            g.dma(xt2[n % 2][0], self.xt_dram(t, 512), xw=xt2[n % 2][1])
            g.dma(ropet[n % 2], T(self.rope_d.ap[:, :, t * 512:t * 512 + 512].rearrange("a p t -> p a t"), self.rope_d.buf))
        loads(0)
        for n, t in enumerate(order):
            s = self.seq_of_tile(t)
            tok = t * 512
            xt = xt2[n % 2][1]
            rt = ropet[n % 2]
            if n + 1 < len(order):
                loads(n + 1)
            g.norm_tile(xt, self.modA[:, li, 0, :, s], self.modT[:, li, 0:8, s], hT, 512, sq, stat, r, tmp)

            def proj(col0, ps_t):
                for kc in range(8):
                    g.mm(ps_t, W[:, kc, col0:col0 + 128], hT[kc], start=(kc == 0), stop=(kc == 7), sig=(kc == 7))
            for c in range(4):
                pt = nps()
                proj(c * 128, pt)
                sg = nstg()
                g.act(sg, pt, AF.Identity, scale=0.125)
                g.dma(T(self.qaT.ap[c, :, tok:tok + 512], self.qaT.buf), sg, part=True)
            for c in range(4):
                pt = nps()
                proj(512 + c * 128, pt)
                sg = nstg()
                g.copy(sg, pt)
                g.dma(T(self.kaT.ap[c, :, tok:tok + 512], self.kaT.buf), sg, part=True)
            for c in range(5):
                isq = c < 4
                col = 1536 + c * 128 if isq else 2048
                cols = 2304 + c * 128 if isq else 2816
                pz = nps()
                proj(col, pz)
                pzs = nps()
                proj(cols, pzs)
                gname, gsname = (f"qn{j}", f"qns{j}") if isq else (f"kn{j}", f"kns{j}")
                g.act(sq2, pz, AF.Square)
                g.mm(stat2, self.bd, sq2)
                g.act(r2, stat2, AF.Ln, scale=1.0 / 64, bias=EPS)
                g.act(r2, r2, AF.Exp, scale=-0.5)
                g.stt(t1, pz, self.pp(gname), r2, ALU.mult, ALU.mult)
                g.stt(t2, pzs, self.pp(gsname), r2, ALU.mult, ALU.mult)
                ci, sidx = (0, 1) if isq else (2, 3)
                g.tt(t1, t1, rt[:, ci, :], ALU.mult)
                g.tt(t2, t2, rt[:, sidx, :], ALU.mult)
                sg = nstg()
                g.tt(sg, t1, t2, ALU.add)
                if isq:
                    g.dma(T(self.qbT.ap[c, :, tok:tok + 512], self.qbT.buf), sg, part=True)
                else:
                    g.dma(T(self.kbT.ap[:, tok:tok + 512], self.kbT.buf), sg, part=True)
            for b4 in range(4):
                pt = nps()
                for kc in range(8):
                    g.mm(pt, hT[kc][:, b4 * 128:(b4 + 1) * 128], W[:, kc, 1024:1536], start=(kc == 0), stop=(kc == 7), sig=(kc == 7))
                sg = nstg()
                g.copy(sg, pt)
                g.dma(self.va[tok + b4 * 128: tok + (b4 + 1) * 128, :], sg, part=True)
                pt = nps()
                for kc in range(8):
                    g.mm(pt[:, 0:128], hT[kc][:, b4 * 128:(b4 + 1) * 128], W[:, kc, 2176:2304], start=(kc == 0), stop=(kc == 7), sig=(kc == 7))
                sg = nstg()
                g.act(sg[:, 0:128], pt[:, 0:128], AF.Copy)
                g.dma(self.vb[tok + b4 * 128: tok + (b4 + 1) * 128, :], sg[:, 0:128], part=True)
            if n == cfg.NTs - 1:
                self.exchange_ab()
        g.stage_reset()

    def allgather(self, loc, gat, rows):
        def cc(e):
            return e.collective_compute("AllGather", ALU.bypass, replica_groups=[list(range(self.cfg.ncores))],
                                        ins=[loc.ap.opt()], outs=[gat.ap[rows:9 * rows, :].opt()])
        self.p.op("gpsimd", cc, reads=[loc], writes=[gat], part=True)

    def fence(self, gats):
        self.nf = getattr(self, "nf", 0) + 1
        ft = self.fence_t[self.nf % 2]
        self.p.dma("gpsimd", lambda e: e.dma_start(out=ft.ap[0:1, 0:16], in_=self.zeros_bf.ap[0:1, 0:16]),
                   reads=[self.zeros_bf.buf] and [self.zeros_bf], writes=[ft] + list(gats), sem=ft, part=True)

    def exchange_ab(self):
        cfg = self.cfg
        g = self
        Ls = cfg.Ls
        tS = cfg.Lp
        g.dma(self.l_kb, self.kbT[:, tS:tS + Ls], sem=self.l_kb)
        for r0 in range(0, Ls, 512):
            g.dma(self.l_vb[r0:r0 + 512, :], self.vb[tS + r0:tS + r0 + 512, :], sem=self.l_vb, part=True)
        for side in range(2):
            t0 = tS if side == 0 else tS + Ls - 256
            for hp in range(4):
                r0 = (side * 4 + hp) * 128
                g.dma(self.l_kah[r0:r0 + 128, :], T(self.kaT.ap[hp, :, t0:t0 + 256], self.kaT.buf), sem=self.l_kah, part=True)
            g.dma(self.l_vah[side * 256:(side + 1) * 256, :], self.va[t0:t0 + 256, :], sem=self.l_vah, part=True)
        g.allgather(self.l_kb, self.g_kb, 128)
        g.allgather(self.l_vb, self.g_vb, Ls)
        g.allgather(self.l_kah, self.g_kah, 1024)
        g.allgather(self.l_vah, self.g_vah, 512)
        g.fence([self.g_kb, self.g_vb, self.g_kah, self.g_vah])

    def stage_B_ab(self, li):
        cfg = self.cfg
        g = self
        j = li // 2
        Lw = cfg.Lmax + 512
        kwin = g.sb([Lw], BF16, name="sb_kwin")
        vwin = g.sb([Lw // 128, 2, 72], BF16, name="sb_vwin")
        qwin = g.sb([cfg.Lmax], BF16, name="sb_qwin")
        bt = g.sb([2, 8, 512], F32, name="sb_bt")
        mq2 = [g.sb([512], BF16, name=f"sb_mq{i}") for i in range(2)]
        khalo = g.sb([4, 2, 256], BF16, name="sb_khalo")
        vhalo = g.sb([2, 2, 512], BF16, name="sb_vhalo")
        tmpb = [g.sb([512], F32, name=f"sb_tmpb{i}") for i in range(2)]
        pT = [g.sb([512], BF16, name=f"sb_pT{i}") for i in range(3)]
        rz = g.sb([512], F32, name="sb_rz")
        o32 = g.sb([512], F32, name="sb_o32")
        ob = [g.sb([512], BF16, name=f"sb_ob{i}") for i in range(2)]
        g.memset(vwin[:, :, :, 64:65], 1.0)
        cnt = 0
        for seq in range(2):
            L = cfg.Lp if seq == 0 else cfg.Ls
            tok0 = 0 if seq == 0 else cfg.Lp
            qt0 = 0 if seq == 0 else cfg.NTp
            nblk = (L + 512) // 128
            for hp in range(4):
                g.dma(kwin[:, 256:256 + L], T(self.kaT.ap[hp, :, tok0:tok0 + L], self.kaT.buf))
                for hh in range(2):
                    c0_ = hp * 128 + hh * 64
                    for t1_ in range(0, L, 1024):
                        g.dma(vwin[:, 2 + t1_ // 128:2 + (t1_ + 1024) // 128, hh, 0:64],
                              T(self.va.ap[tok0 + t1_:tok0 + t1_ + 1024, c0_:c0_ + 64].rearrange("(b p) d -> p b d", p=128), self.va.buf), part=True)
                if seq == 0:
                    g.memset(kwin[:, 0:256], 0.0, part=True)
                    g.memset(kwin[:, 256 + L:512 + L], 0.0, part=True)
                    g.memset(vwin[:, 0:2, :, 0:64], 0.0, part=True)
                    g.memset(vwin[:, 2 + L // 128:4 + L // 128, :, 0:64], 0.0, part=True)
                else:
                    if hp == 0:
                        gk, gv = self.g_kah, self.g_vah
                        g.dyn_dma(khalo[:, :, 0, :], lambda v: gk.ap[512:, :][bass.ds(v, 512), :].rearrange("(h p) t -> p h t", p=128), gk, "L1024", part=True)
                        g.dyn_dma(khalo[:, :, 1, :], lambda v: gk.ap[bass.ds(v, 512), :].rearrange("(h p) t -> p h t", p=128), gk, "R1024", part=True)
                        g.dyn_dma(vhalo[:, 0], lambda v: gv.ap[256:, :][bass.ds(v, 256), :].rearrange("(b p) c -> p b c", p=128), gv, "L512", part=True)
                        g.dyn_dma(vhalo[:, 1], lambda v: gv.ap[bass.ds(v, 256), :].rearrange("(b p) c -> p b c", p=128), gv, "R512", part=True)
                    g.copy(kwin[:, 0:256], khalo[:, hp, 0, :], part=True)
                    g.copy(kwin[:, 256 + L:512 + L], khalo[:, hp, 1, :], part=True)
                    for hh in range(2):
                        c0_ = hp * 128 + hh * 64
                        g.copy(vwin[:, 0:2, hh, 0:64], vhalo[:, 0, :, c0_:c0_ + 64], part=True)
                        g.copy(vwin[:, 2 + L // 128:4 + L // 128, hh, 0:64], vhalo[:, 1, :, c0_:c0_ + 64], part=True)
                g.dma(qwin[:, 0:L], T(self.qaT.ap[hp, :, tok0:tok0 + L], self.qaT.buf))
                g.dma(bt, T(self.bt_d.ap[j, 2 * hp:2 * hp + 2].rearrange("h b p q -> p h b q"), self.bt_d.buf))
                for qt in range(L // 512):
                    mqt = mq2[(qt + hp) % 2]
                    g.dma(mqt[0:16, :], self.mq_d[:, (qt0 + qt) * 512:(qt0 + qt + 1) * 512], eng="gpsimd")
                    for hh in range(2):
                        cnt += 1
                        acc = self.ps[3 + cnt % 2]
                        pr = slice(64 * hh, 64 * hh + 64)
                        for blk in range(8):
                            S = self.ps[blk % 3]
                            k0 = 512 * qt + 128 * blk
                            g.mm(S, kwin[pr, k0:k0 + 128], qwin[pr, 512 * qt:512 * qt + 512], start=True, stop=False)
                            g.mm(S, self.e16[0:16, 128 * blk:128 * blk + 128], mqt[0:16, :], start=False, stop=True)
                            tb = tmpb[blk % 2]
                            g.tt(tb, S, bt[:, hh, blk, :], ALU.add)
                            p_ = pT[blk % 3]
                            g.act(p_, tb, AF.Exp)
                            g.mm(acc[0:65, :], vwin[:, 4 * qt + blk, hh, 0:65], p_, start=(blk == 0), stop=(blk == 7))
                        o = ob[cnt % 2]
                        g.fin_attn(acc, [o[0:64, :]], rz, o32)
                        h = 2 * hp + hh
                        tk = tok0 + 512 * qt
                        g.dma(T(self.aoT.ap[h // 2, 64 * (h % 2):64 * (h % 2) + 64, tk:tk + 512], self.aoT.buf), o[0:64, :], part=True)
        g.stage_reset()
        NKmax = max(cfg.Lp, 8 * cfg.Ls)
        kk = g.sb([NKmax], BF16, name="sb_kk")
        vv = g.sb([NKmax // 128, 72], BF16, name="sb_vv")
        qch = g.sb([cfg.Lmax], BF16, name="sb_qch")
        pT = [g.sb([512], BF16, name=f"sb_pT{i}") for i in range(3)]
        rz = g.sb([512], F32, name="sb_rz")
        o32 = g.sb([512], F32, name="sb_o32")
        ob = [g.sb([512], BF16, name=f"sb_ob{i}") for i in range(2)]
        g.memset(vv[:, :, 64:65], 1.0)
        cnt = 0
        for seq in range(2):
            L = cfg.Lp if seq == 0 else cfg.Ls
            NK = cfg.Lp if seq == 0 else 8 * cfg.Ls
            tok0 = 0 if seq == 0 else cfg.Lp
            for kv in range(2):
                for half in range(2):
                    pr = slice(64 * half, 64 * half + 64)
                    if seq == 0:
                        g.dma(kk[pr, 0:NK], self.kbT[64 * kv:64 * kv + 64, 0:NK], part=True)
                    else:
                        for rk in range(8):
                            g.dma(kk[pr, rk * cfg.Ls:(rk + 1) * cfg.Ls], self.g_kb[(1 + rk) * 128 + 64 * kv:(1 + rk) * 128 + 64 * kv + 64, :], part=True)
                vsrc = self.vb.ap[0:NK, :] if seq == 0 else self.g_vb.ap[cfg.Ls:cfg.Ls + NK, :]
                vbuf = self.vb.buf if seq == 0 else self.g_vb.buf
                for c0 in range(0, NK, 1024):
                    c1 = min(NK, c0 + 1024)
                    g.dma(vv[:, c0 // 128:c1 // 128, 0:64], T(vsrc[c0:c1, 64 * kv:64 * kv + 64].rearrange("(b p) d -> p b d", p=128), vbuf), part=True)
                for cp in range(2):
                    chunk = 2 * kv + cp
                    g.dma(qch[:, 0:L], T(self.qbT.ap[chunk, :, tok0:tok0 + L], self.qbT.buf))
                    for hh in range(2):
                        pr = slice(64 * hh, 64 * hh + 64)
                        h = 2 * chunk + hh
                        for qt in range(L // 512):
                            cnt += 1
                            acc = self.ps[3 + cnt % 2]
                            nb = NK // 128
                            for blk in range(nb):
                                S = self.ps[blk % 3]
                                g.mm(S, kk[pr, blk * 128:(blk + 1) * 128], qch[pr, 512 * qt:512 * qt + 512])
                                p_ = pT[blk % 3]
                                g.act(p_, S, AF.Exp)
                                g.mm(acc[0:65, :], vv[:, blk, 0:65], p_, start=(blk == 0), stop=(blk == nb - 1))
                            o = ob[cnt % 2]
                            g.fin_attn(acc, [o[0:64, :]], rz, o32)
                            tk = tok0 + 512 * qt
                            g.dma(T(self.aoT.ap[4 + h // 2, 64 * (h % 2):64 * (h % 2) + 64, tk:tk + 512], self.aoT.buf), o[0:64, :], part=True)
        g.stage_reset()

    def stage_A_cd(self, li):
        cfg = self.cfg
        g = self
        j = li // 2
        W = g.sb([8, 2560], BF16, name="sb_Wcd")
        g.load_weight(W, self.w_in_cd.ap[j], self.w_in_cd.buf, 8)
        xt2 = [g.sbc(8, 512, F32, f"sb_xt{i}") for i in range(2)]
        hT = [g.sb([512], BF16, name=f"sb_hT{k}") for k in range(8)]
        sq = [g.sb([512], F32, name=f"sb_sq{i}") for i in range(2)]
        tmp = [g.sb([512], F32, name=f"sb_tmp{i}") for i in range(2)]
        r = g.sb([512], F32, name="sb_r")
        stg = [g.sb([512], BF16, name=f"sb_stg{i}") for i in range(6)]
        sgm = g.sb([512], F32, name="sb_sgm")
        stat = self.ps[7]
        order = list(range(cfg.NTp, cfg.NT)) + list(range(cfg.NTp))
        si = [0]
        pi = [0]

        def nstg():
            si[0] += 1
            return stg[si[0] % 6]

        def nps():
            pi[0] += 1
            return self.ps[pi[0] % 7]

        g.dma(xt2[0][0], self.xt_dram(order[0], 512), xw=xt2[0][1])
        for n, t in enumerate(order):
            s = self.seq_of_tile(t)
            tok = t * 512
            xt = xt2[n % 2][1]
            if n + 1 < len(order):
                g.dma(xt2[(n + 1) % 2][0], self.xt_dram(order[n + 1], 512), xw=xt2[(n + 1) % 2][1])
            g.norm_tile(xt, self.modA[:, li, 0, :, s], self.modT[:, li, 0:8, s], hT, 512, sq, stat, r, tmp)

            def proj(col0, ps_t):
                for kc in range(8):
                    g.mm(ps_t, W[:, kc, col0:col0 + 128], hT[kc], start=(kc == 0), stop=(kc == 7), sig=(kc == 7))
            for c in range(4):
                pa = nps()
                proj(c * 128, pa)
                pg = nps()
                proj(512 + c * 128, pg)
                g.act(sgm, pg, AF.Sigmoid)
                sg = nstg()
                g.tt(sg, sgm, pa, ALU.mult)
                g.dma(T(self.uT.ap[c, :, tok:tok + 512], self.uT.buf), sg, part=True)
            for c in range(4):
                pt = nps()
                proj(1024 + c * 128, pt)
                sg = nstg()
                g.act(sg, pt, AF.Identity, scale=0.125)
                g.dma(T(self.qdT.ap[c, :, tok:tok + 512], self.qdT.buf), sg, part=True)
            for c in range(4):
                pt = nps()
                proj(1536 + c * 128, pt)
                sg = nstg()
                g.copy(sg, pt)
                g.dma(T(self.kdT.ap[c, :, tok:tok + 512], self.kdT.buf), sg, part=True)
            for b4 in range(4):
                pt = nps()
                for kc in range(8):
                    g.mm(pt, hT[kc][:, b4 * 128:(b4 + 1) * 128], W[:, kc, 2048:2560], start=(kc == 0), stop=(kc == 7), sig=(kc == 7))
                sg = nstg()
                if b4 % 2 == 0:
                    g.copy(sg, pt)
                else:
                    g.act(sg, pt, AF.Copy)
                g.dma(self.vd[tok + b4 * 128: tok + (b4 + 1) * 128, :], sg, part=True)
            if n == cfg.NTs - 1:
                self.exchange_cd()
        g.stage_reset()

    def exchange_cd(self):
        cfg = self.cfg
        g = self
        Ls = cfg.Ls
        tS = cfg.Lp
        for h in range(4):
            g.dma(self.l_kd[h * 128:(h + 1) * 128, :], T(self.kdT.ap[h, :, tS:tS + Ls], self.kdT.buf), sem=self.l_kd, part=True)
        for r0 in range(0, Ls, 512):
            g.dma(self.l_vd[r0:r0 + 512, :], self.vd[tS + r0:tS + r0 + 512, :], sem=self.l_vd, part=True)
        for side in range(2):
            t0 = tS if side == 0 else tS + Ls - 16
            for c in range(4):
                r0 = (side * 4 + c) * 128
                g.dma(self.l_uh[r0:r0 + 128, :], T(self.uT.ap[c, :, t0:t0 + 16], self.uT.buf), sem=self.l_uh, part=True)
        g.allgather(self.l_kd, self.g_kd, 512)
        g.allgather(self.l_vd, self.g_vd, Ls)
        g.allgather(self.l_uh, self.g_uh, 1024)
        g.fence([self.g_kd, self.g_vd, self.g_uh])

    def stage_B_cd(self, li):
        cfg = self.cfg
        g = self
        j = li // 2
        uw = [g.sb([cfg.Lmax + 32], BF16, name=f"sb_uw{c}") for c in range(4)]
        cv = [g.sb([512], F32, name=f"sb_cv{c}") for c in range(4)]
        uhalo = g.sb([4, 2, 16], BF16, name="sb_uhalo")
        sq = [g.sb([512], F32, name=f"sb_sq{i}") for i in range(2)]
        mean = g.sb([512], F32, name="sb_mean")
        var = g.sb([512], F32, name="sb_var")
        msq = g.sb([512], F32, name="sb_msq")
        tn = [g.sb([512], F32, name=f"sb_tn{i}") for i in range(2)]
        stg = [g.sb([512], BF16, name=f"sb_stg{i}") for i in range(4)]
        s1, s2 = self.ps[6], self.ps[7]
        n = 0
        for seq in range(2):
            L = cfg.Lp if seq == 0 else cfg.Ls
            tok0 = 0 if seq == 0 else cfg.Lp
            for c in range(4):
                g.dma(uw[c][:, 16:16 + L], T(self.uT.ap[c, :, tok0:tok0 + L], self.uT.buf))
                if seq == 0:
                    g.memset(uw[c][:, 0:16], 0.0, part=True)
                    g.memset(uw[c][:, 16 + L:32 + L], 0.0, part=True)
                else:
                    if c == 0:
                        gu = self.g_uh
                        g.dyn_dma(uhalo[:, :, 0, :], lambda v: gu.ap[512:, :][bass.ds(v, 512), :].rearrange("(c p) t -> p c t", p=128), gu, "L1024", part=True)
                        g.dyn_dma(uhalo[:, :, 1, :], lambda v: gu.ap[bass.ds(v, 512), :].rearrange("(c p) t -> p c t", p=128), gu, "R1024", part=True)
                    g.copy(uw[c][:, 0:16], uhalo[:, c, 0, :], part=True)
                    g.copy(uw[c][:, 16 + L:32 + L], uhalo[:, c, 1, :], part=True)
            for tt_ in range(L // 512):
                t0 = tt_ * 512
                for c in range(4):
                    g.ts(cv[c], uw[c][:, t0 + 1:t0 + 513], self.pp(f"convw{j}", c * 31), ALU.mult, self.pp(f"convb{j}", c), ALU.add)
                for tap in range(1, 31):
                    for c in range(4):
                        g.stt(cv[c], uw[c][:, t0 + 1 + tap:t0 + 513 + tap], self.pp(f"convw{j}", c * 31 + tap), cv[c], ALU.mult, ALU.add)
                for c in range(4):
                    g.mm(s1, self.onesf, cv[c], start=(c == 0), stop=(c == 3))
                    g.act(sq[c % 2], cv[c], AF.Square)
                    g.mm(s2, self.onesf, sq[c % 2], start=(c == 0), stop=(c == 3))
                g.act(mean, s1, AF.Identity, scale=1.0 / 512)
                g.act(msq, mean, AF.Square)
                g.stt(var, s2, 1.0 / 512, msq, ALU.mult, ALU.subtract)
                g.act(var, var, AF.Ln, bias=LN_EPS)
                g.act(var, var, AF.Exp, scale=-0.5)
                for c in range(4):
                    n += 1
                    tq = tn[n % 2]
                    g.tt(tq, cv[c], mean, ALU.subtract)
                    g.tt(tq, tq, var, ALU.mult)
                    sg = stg[n % 4]
                    g.act(sg, tq, AF.Silu, scale=self.pp(f"lng{j}", c), bias=self.pp(f"lnb{j}", c))
                    tk = tok0 + t0
                    g.dma(T(self.aoT.ap[c, :, tk:tk + 512], self.aoT.buf), sg, part=True)
        g.stage_reset()
        Lm = cfg.Lmax
        NKg = 8 * cfg.Ls
        kl = [g.sb([Lm], BF16, name=f"sb_kl{m}") for m in range(2)]
        kg = [g.sb([NKg], BF16, name=f"sb_kg{m}") for m in range(2)]
        qB = [[g.sb([512], BF16, name=f"sb_qB{m}{i}") for m in range(2)] for i in range(2)]
        qA = [[g.sb([512], BF16, name=f"sb_qA{m}{i}") for m in range(2)] for i in range(2)]
        vl = g.sb([Lm // 128, 144], BF16, name="sb_vl")
        vg = g.sb([NKg // 128, 144], BF16, name="sb_vg")
        dpt = g.sb([4, 512], F32, name="sb_dpt")
        tmpb = [g.sb([512], F32, name=f"sb_tmpb{i}") for i in range(2)]
        pT = [g.sb([512], BF16, name=f"sb_pT{i}") for i in range(3)]
        rz = [g.sb([512], F32, name=f"sb_rz{i}") for i in range(2)]
        oa = [g.sb([512], F32, name=f"sb_oa{i}") for i in range(2)]
        obb = [g.sb([512], F32, name=f"sb_obb{i}") for i in range(2)]
        fa = g.sb([512], F32, name="sb_fa")
        fb = g.sb([512], F32, name="sb_fb")
        sqa = g.sb([512], F32, name="sb_sqa")
        sqb = g.sb([512], F32, name="sb_sqb")
        rr = g.sb([512], F32, name="sb_rr")
        outa = [g.sb([512], BF16, name=f"sb_outa{i}") for i in range(2)]
        outb = [g.sb([512], BF16, name=f"sb_outb{i}") for i in range(2)]
        g.memset(vl[:, :, 64:65], 1.0)
        g.memset(vg[:, :, 64:65], 1.0)
        for m in range(2):
            g.dma(kl[m][64:68, 0:Lm], self.kaugl_d, eng="gpsimd", part=True)
            g.dma(kg[m][64:68, 0:NKg], self.kaugr_d, eng="gpsimd", part=True)
        cnt = 0
        for seq in range(2):
            L = cfg.Lp if seq == 0 else cfg.Ls
            tok0 = 0 if seq == 0 else cfg.Lp
            for h in range(4):
                for m in range(2):
                    pr = slice(64 * m, 64 * m + 64)
                    g.dma(kl[m][0:64, 0:L], T(self.kdT.ap[h, pr, tok0:tok0 + L], self.kdT.buf), part=True)
                    if seq == 1:
                        for rk in range(8):
                            r0 = (1 + rk) * 512 + h * 128 + 64 * m
                            g.dma(kg[m][0:64, rk * cfg.Ls:(rk + 1) * cfg.Ls], self.g_kd[r0:r0 + 64, :], part=True)
                vsrcs = [(vl, self.vd.ap[tok0:tok0 + L, :], self.vd.buf, L)]
                if seq == 1:
                    vsrcs.append((vg, self.g_vd.ap[cfg.Ls:cfg.Ls + NKg, :], self.g_vd.buf, NKg))
                for (vt, src, sbuf, n_) in vsrcs:
                    for c0 in range(0, n_, 1024):
                        c1 = min(n_, c0 + 1024)
                        g.dma(vt[:, c0 // 128:c1 // 128, 0:64], T(src[c0:c1, h * 128:h * 128 + 64].rearrange("(b p) d -> p b d", p=128), sbuf), part=True)
                        g.dma(vt[:, c0 // 128:c1 // 128, 72:136], T(src[c0:c1, h * 128 + 64:h * 128 + 128].rearrange("(b p) d -> p b d", p=128), sbuf), part=True)
                g.dma(dpt, T(self.dp_d.ap[h].rearrange("a p q -> p a q"), self.dp_d.buf))
                for qt in range(L // 512):
                    cnt += 1
                    qs = slice(512 * qt, 512 * qt + 512)
                    qBq, qAq = qB[cnt % 2], qA[cnt % 2]
                    for m in range(2):
                        pr = slice(64 * m, 64 * m + 64)
                        src = T(self.qdT.ap[h, pr, tok0 + 512 * qt:tok0 + 512 * qt + 512], self.qdT.buf)
                        g.dma(qBq[m][0:64, :], src, part=True)
                        g.dma(qAq[m][0:64, :], src, part=True)
                        g.dma(qBq[m][64:68, :], self.qaug_d[:, (2 * h) * Lm + 512 * qt:(2 * h) * Lm + 512 * qt + 512], eng="gpsimd", part=True)
                        g.dma(qAq[m][64:68, :], self.qaug_d[:, (2 * h + 1) * Lm + 512 * qt:(2 * h + 1) * Lm + 512 * qt + 512], eng="gpsimd", part=True)
                    for m in range(2):
                        acc_a = self.ps[3 + m]
                        acc_b = self.ps[5 + m]
                        blocks = [("l", b) for b in range(L // 128)] + ([("g", b) for b in range(NKg // 128)] if seq == 1 else [])
                        nb = len(blocks)
                        for bi, (kind, b) in enumerate(blocks):
                            S = self.ps[bi % 3]
                            ks = slice(128 * b, 128 * b + 128)
                            p_ = pT[bi % 3]
                            if kind == "g":
                                g.mm(S, kg[m][0:68, ks], qBq[m][0:68, :])
                                g.act(p_, S, AF.Exp)
                                v_ = vg
                            else:
                                v_ = vl
                                if b < 4 * qt:
                                    g.mm(S, kl[m][0:68, ks], qBq[m][0:68, :])
                                    g.act(p_, S, AF.Exp)
                                elif b >= 4 * qt + 4:
                                    g.mm(S, kl[m][0:68, ks], qAq[m][0:68, :])
                                    g.act(p_, S, AF.Exp)
                                else:
                                    g.mm(S, kl[m][0:68, ks], qBq[m][0:68, :])
                                    tb = tmpb[bi % 2]
                                    g.tt(tb, S, dpt[:, b - 4 * qt, :], ALU.add)
                                    g.act(p_, tb, AF.Exp)
                            g.mm(acc_a[0:65, :], v_[:, b, 0:65], p_, start=(bi == 0), stop=(bi == nb - 1))
                            g.mm(acc_b[0:64, :], v_[:, b, 72:136], p_, start=(bi == 0), stop=(bi == nb - 1))
                        g.recip(rz[m][64:65, :], acc_a[64:65, :])
                        g.act(oa[m][0:64, :], acc_a[0:64, :], AF.Copy)
                        g.act(obb[m][0:64, :], acc_b[0:64, :], AF.Copy)
                    bc = self.ps[7]
                    for m in range(2):
                        g.mm(bc[0:64, :], self.onesf[64:65, 0:64], rz[m][64:65, :])
                        g.tt(oa[m][0:64, :], oa[m][0:64, :], bc[0:64, :], ALU.mult)
                        g.tt(obb[m][0:64, :], obb[m][0:64, :], bc[0:64, :], ALU.mult)
                    nl = self.lam[0:64, j, 1:2]
                    g.stt(fa[0:64, :], oa[1][0:64, :], nl, oa[0][0:64, :], ALU.mult, ALU.add)
                    g.stt(fb[0:64, :], obb[1][0:64, :], nl, obb[0][0:64, :], ALU.mult, ALU.add)
                    g.act(sqa[0:64, :], fa[0:64, :], AF.Square)
                    g.act(sqb[0:64, :], fb[0:64, :], AF.Square)
                    stp = self.ps[7]
                    g.mm(stp[0:64, :], self.onesf[0:64, 0:64], sqa[0:64, :], start=True, stop=False)
                    g.mm(stp[0:64, :], self.onesf[0:64, 0:64], sqb[0:64, :], start=False, stop=True)
                    g.act(rr[0:64, :], stp[0:64, :], AF.Ln, scale=1.0 / 128, bias=EPS)
                    g.act(rr[0:64, :], rr[0:64, :], AF.Exp, scale=-0.5)
                    oa_, ob_ = outa[cnt % 2], outb[cnt % 2]
                    g.stt(oa_[0:64, :], fa[0:64, :], self.subg[0:64, j, 0:1], rr[0:64, :], ALU.mult, ALU.mult)
                    g.stt(ob_[0:64, :], fb[0:64, :], self.subg[0:64, j, 1:2], rr[0:64, :], ALU.mult, ALU.mult)
                    tk = tok0 + 512 * qt
                    g.dma(T(self.aoT.ap[4 + h, 0:64, tk:tk + 512], self.aoT.buf), oa_[0:64, :], part=True)
                    g.dma(T(self.aoT.ap[4 + h, 64:128, tk:tk + 512], self.aoT.buf), ob_[0:64, :], part=True)
        g.stage_reset()

    def stage_CD(self, li):
        cfg = self.cfg
        g = self
        j = li // 2
        TF = cfg.TF
        Wo = g.sb([8, D], BF16, name="sb_Wo")
        W1 = g.sb([8, DFF], BF16, name="sb_W1")
        W3 = g.sb([8, DFF], BF16, name="sb_W3")
        W2 = g.sb([FC, D], BF16, name="sb_W2")
        wsrc = (self.w_out_ab if li % 2 == 0 else self.w_out_cd)
        g.load_weight(Wo, wsrc.ap[j], wsrc.buf, 8)
        g.load_weight(W1, self.w1.ap[li], self.w1.buf, 8)
        g.load_weight(W3, self.w3.ap[li], self.w3.buf, 8)
        g.load_weight(W2, self.w2.ap[li], self.w2.buf, FC)
        xt2 = [g.sbc(8, TF, F32, f"sb_xt{i}") for i in range(2)]
        ao1 = g.sb([8, TF], BF16, name="sb_ao")
        hT = [g.sb([TF], BF16, name=f"sb_hT{k}") for k in range(8)]
        actT = [g.sb([TF], BF16, name=f"sb_act{k}") for k in range(FC)]
        sq = [g.sb([TF], F32, name=f"sb_sq{i}") for i in range(2)]
        tmp = [g.sb([TF], F32, name=f"sb_tmp{i}") for i in range(2)]
        sl = [g.sb([TF], F32, name=f"sb_sl{i}") for i in range(2)]
        r = g.sb([TF], F32, name="sb_r")
        hb = []
        for b in range(7):
            hb.append(self.ps[b][:, 0:TF])
            hb.append(self.ps[b][:, 256:256 + TF])
        stat = self.ps[7]
        pi = [0]

        def nps():
            pi[0] += 1
            return hb[pi[0] % 14]
        ntile = cfg.NTOK // TF
        def loads(t):
            g.dma(xt2[t % 2][0], self.xt_dram(t, TF), xw=xt2[t % 2][1])
            g.dma(ao1, T(self.aoT.ap[:, :, t * TF:t * TF + TF].rearrange("k p t -> p k t"), self.aoT.buf))
        loads(0)
        for t in range(ntile):
            tok = t * TF
            s = 0 if tok < cfg.Lp else 1
            xtw, xt = xt2[t % 2]
            ao = ao1
            for oc in range(8):
                pt = nps()
                for kc in range(8):
                    g.mm(pt, Wo[:, kc, oc * 128:(oc + 1) * 128], ao[:, kc, :], start=(kc == 0), stop=(kc == 7), sig=(kc == 7))
                g.stt(xt[oc], pt, self.modG[:, li, 0, oc, s:s + 1], xt[oc], ALU.mult, ALU.add)
            if t + 1 < ntile:
                loads(t + 1)
            g.norm_tile(xt, self.modA[:, li, 1, :, s], self.modT[:, li, 24:32, s], hT, TF, sq, stat, r, tmp)
            for fc in range(FC):
                p1 = nps()
                for kc in range(8):
                    g.mm(p1, W1[:, kc, fc * 128:(fc + 1) * 128], hT[kc], start=(kc == 0), stop=(kc == 7), sig=(kc == 7))
                p3 = nps()
                for kc in range(8):
                    g.mm(p3, W3[:, kc, fc * 128:(fc + 1) * 128], hT[kc], start=(kc == 0), stop=(kc == 7), sig=(kc == 7))
                g.act(sl[fc % 2], p1, AF.Silu)
                g.tt(actT[fc], sl[fc % 2], p3, ALU.mult)
            for oc in range(8):
                pt = nps()
                for fc in range(FC):
                    g.mm(pt, W2[:, fc, oc * 128:(oc + 1) * 128], actT[fc], start=(fc == 0), stop=(fc == FC - 1), sig=(fc == FC - 1))
                g.stt(xt[oc], pt, self.modG[:, li, 1, oc, s:s + 1], xt[oc], ALU.mult, ALU.add)
            g.dma(self.xt_dram(t, TF), xtw, xr=xt)
        g.stage_reset()

    def stage_final(self):
        cfg = self.cfg
        g = self
        xt2 = [g.sbc(8, 512, F32, f"sb_xt{i}") for i in range(2)]
        yT = [g.sb([512], F32, name=f"sb_yT{k}") for k in range(8)]
        sq = [g.sb([512], F32, name=f"sb_sq{i}") for i in range(2)]
        tmp = [g.sb([512], F32, name=f"sb_tmp{i}") for i in range(2)]
        r = g.sb([512], F32, name="sb_r")
        yo = [g.sb([D], F32, name=f"sb_yo{i}") for i in range(2)]
        stat = self.ps[7]
        gf = self.pp("gfin", 0, 8)
        n = 0
        for t in range(cfg.NT):
            xtw, xt = xt2[t % 2]
            g.dma(xtw, self.xt_dram(t, 512), xw=xt)
            g.norm_tile(xt, gf, None, yT, 512, sq, stat, r, tmp)
            for b4 in range(4):
                n += 1
                o = yo[n % 2]
                for half in range(2):
                    pt = self.ps[(2 * n + half) % 6]
                    for q in range(4):
                        kc = half * 4 + q
                        g.tr(pt[:, q * 128:(q + 1) * 128], yT[kc][:, b4 * 128:(b4 + 1) * 128], self.ident)
                    if half == 0:
                        g.copy(o[:, 0:512], pt, part=True)
                    else:
                        g.act(o[:, 512:1024], pt, AF.Copy, part=True)
                tok = t * 512 + b4 * 128
                g.dma(self.yout[tok:tok + 128, :], o)


def _rope_tables(pos):
    half = 32
    inv = (10000.0 ** (-np.arange(0, half, 2, dtype=np.float32) / half)).astype(np.float32)
    row = (pos // 64).astype(np.float32)
    col = (pos % 64).astype(np.float32)
    ang = np.concatenate([row[:, None] * inv, col[:, None] * inv], axis=-1).astype(np.float32)
    cos = np.cos(ang).astype(np.float32)
    sin = np.sin(ang).astype(np.float32)
    d = np.arange(128) % 64
    pair = d // 2
    cosf = cos[:, pair].T
    sgn = np.where(d % 2 == 0, -1.0, 1.0).astype(np.float32)
    sinf = (sin[:, pair] * sgn[None, :]).T
    return cosf.astype(np.float32), sinf.astype(np.float32)


def _mq_table(rows_glob, row0, nqt):
    out = np.zeros((16, nqt * 512), np.float32)
    for qt in range(nqt):
        for qr in range(8):
            rg = row0 + 8 * qt + qr
            rs = min(max(rg - 4, 0), rows_glob - 8)
            for jj in range(16):
                kr = row0 + 8 * qt - 4 + jj
                ok = rs <= kr < rs + 8
                if not ok:
                    out[jj, qt * 512 + qr * 64: qt * 512 + qr * 64 + 64] = NEG
    return out


def _bt_tables(rpb):
    nab = rpb.shape[0]
    i = np.arange(128)
    jq = np.arange(512)
    kcol = (i % 64)[:, None]
    qc = (jq % 64)[None, :]
    qr = (jq // 64)[None, :]
    cs = np.clip(qc - 8, 0, 48)
    colok = (kcol >= cs) & (kcol < cs + 16)
    dcol = np.clip(kcol - qc + 15, 0, 30)
    out = np.empty((nab, 8, 8, 128, 512), np.float32)
    ext = np.concatenate([rpb.reshape(nab, 8, -1), np.full((nab, 8, 1), NEG, np.float32)], axis=-1)
    for blk in range(8):
        jj = 2 * blk + (i // 64)[:, None]
        dr = jj - 4 - qr
        ok = colok & (np.abs(dr) <= 7)
        idx = np.where(ok, (np.clip(dr + 7, 0, 14)) * 31 + dcol, 15 * 31)
        out[:, :, blk] = ext[:, :, idx]
    return out


def _split_pos(s):
    a = (s // 128) * 128
    return a.astype(np.float32), (s - a).astype(np.float32)


def prepare_inputs(cfg, inp):
    Lp, Ls, NTOK, depth = cfg.Lp, cfg.Ls, cfg.NTOK, cfg.depth
    f32 = np.float32
    ppoff, NPP = pp_layout(cfg)
    pp = np.zeros((128, NPP), f32)

    def put_feat(name, vec, nch):
        o, w = ppoff[name]
        pp[:, o:o + nch] = np.asarray(vec, f32).reshape(nch, 128).T
    for li in range(depth):
        put_feat(f"gmix{li}", inp["norm_mix_g"][li], 8)
        put_feat(f"gffn{li}", inp["norm_ffn_g"][li], 8)
        put_feat(f"bmod{li}", inp["b_mod"][li], 48)
    put_feat("gfin", inp["final_g"], 8)
    sw = np.arange(64) ^ 1
    for j in range(cfg.NAB):
        qn = np.asarray(inp["qnorm_b"][j], f32)
        kn = np.asarray(inp["knorm_b"][j], f32)
        for nm, v in (("qn", qn), ("qns", qn[sw]), ("kn", kn), ("kns", kn[sw])):
            o, _ = ppoff[f"{nm}{j}"]
            pp[:, o] = np.tile(v, 2)
    for j in range(cfg.NCD):
        o, _ = ppoff[f"convw{j}"]
        cw = np.asarray(inp["conv_w_c"][j], f32)
        pp[:, o:o + 124] = cw.T.reshape(4, 128, 31).transpose(1, 0, 2).reshape(128, 124)
        put_feat(f"convb{j}", inp["conv_b_c"][j], 4)
        put_feat(f"lng{j}", inp["conv_ln_g"][j], 4)
        put_feat(f"lnb{j}", inp["conv_ln_b"][j], 4)
        o, _ = ppoff[f"subg{j}"]
        sg = np.asarray(inp["subln_g"][j], f32)
        pp[:, o] = np.tile(sg[0:64], 2)
        pp[:, o + 1] = np.tile(sg[64:128], 2)
        for nm, key in (("lq1", "lam_q1"), ("lk1", "lam_k1"), ("lq2", "lam_q2"), ("lk2", "lam_k2")):
            o, _ = ppoff[f"{nm}{j}"]
            pp[:, o:o + 64] = np.asarray(inp[key][j], f32)[None, :]
    ident = np.eye(128, dtype=f32)
    bd = np.kron(np.eye(2, dtype=f32), np.ones((64, 64), f32))
    e16 = np.zeros((16, 1024), f32)
    for jj in range(16):
        e16[jj, jj * 64:(jj + 1) * 64] = 1.0
    wab = np.asarray(inp["w_in_ab"], f32)
    qb = wab[:, :, 1536:2048].reshape(wab.shape[0], D, 8, 64)[..., sw].reshape(wab.shape[0], D, 512)
    kb = wab[:, :, 2048:2176].reshape(wab.shape[0], D, 2, 64)[..., sw].reshape(wab.shape[0], D, 128)
    wab_ext = np.ascontiguousarray(np.concatenate([wab, qb, kb], axis=-1))
    bt = _bt_tables(np.asarray(inp["rpb_a"], f32))
    Lm = cfg.Lmax
    s = np.arange(Lm)
    sa, sb_ = _split_pos(s)
    kaugl = np.stack([sa, sb_, np.ones(Lm, f32), np.ones(Lm, f32)]).astype(f32)
    qaug = np.zeros((4, 8 * Lm), f32)
    for h in range(4):
        m = SLOPES[h]
        B = np.stack([np.full(Lm, m, f32), np.full(Lm, m, f32), -m * sa, -m * sb_]).astype(f32)
        qaug[:, (2 * h) * Lm:(2 * h + 1) * Lm] = B
        qaug[:, (2 * h + 1) * Lm:(2 * h + 2) * Lm] = -B
    dpt = np.zeros((4, 4, 128, 512), f32)
    ii = np.arange(128)[:, None]
    jj_ = np.arange(512)[None, :]
    for h in range(4):
        for d_ in range(4):
            val = 128 * d_ + ii - jj_
            dpt[h, d_] = np.where(val > 0, -2.0 * SLOPES[h] * val, 0.0)
    common = dict(ppvec=pp, ident=ident, bd=bd, e16=e16, kaugl=kaugl, qaug=qaug, dpt=dpt, bt=bt,
                  w_mod=np.asarray(inp["w_mod"], f32), w_in_ab=wab_ext, w_out_ab=np.asarray(inp["w_out_ab"], f32),
                  w_in_cd=np.asarray(inp["w_in_cd"], f32), w_out_cd=np.asarray(inp["w_out_cd"], f32),
                  w1=np.asarray(inp["w1"], f32), w3=np.asarray(inp["w3"], f32), w2=np.asarray(inp["w2"], f32))
    xp = np.asarray(inp["x_prompt"], f32)
    xs = np.asarray(inp["x_sample"], f32)
    cp = np.asarray(inp["c_prompt"], f32)
    cs_ = np.asarray(inp["c_sample"], f32)
    rows_s_glob = (8 * Ls) // 64
    maps = []
    for c in range(cfg.ncores):
        d = dict(common)
        d["xin"] = np.ascontiguousarray(np.concatenate([xp[c], xs[0, c * Ls:(c + 1) * Ls]], axis=0))
        d["cin"] = np.ascontiguousarray(np.stack([cp[c], cs_[0]]))
        pos = np.concatenate([np.arange(Lp), c * Ls + np.arange(Ls)])
        cosf, sinf = _rope_tables(pos)
        d["rope"] = np.ascontiguousarray(np.stack([cosf * 0.125, sinf * 0.125, cosf, sinf]).astype(f32))
        d["mq"] = np.ascontiguousarray(np.concatenate([_mq_table(Lp // 64, 0, cfg.NTp), _mq_table(rows_s_glob, c * (Ls // 64), cfg.NTs)], axis=1))
        kr = np.zeros((4, 8 * Ls), f32)
        sl_ = np.arange(Ls)
        la, lb = _split_pos(sl_)
        for rk in range(8):
            if rk == c:
                blk = np.stack([np.full(Ls, -65536.0, f32), np.zeros(Ls, f32), np.zeros(Ls, f32), np.zeros(Ls, f32)])
            else:
                sg = 1.0 if rk < c else -1.0
                blk = sg * np.stack([(rk - c) * Ls + la, lb, np.ones(Ls, f32), np.ones(Ls, f32)])
            kr[:, rk * Ls:(rk + 1) * Ls] = blk
        d["kaugr"] = kr
        d["info"] = np.array([[c * 1024, (c + 2) * 1024, c * 512, (c + 2) * 512, 0, 0, 0, 0]], np.int32)
        maps.append(d)
    return maps


_CACHE = {}


def run(cfg, inp):
    key = (cfg.Lp, cfg.Ls, cfg.depth)
    if key not in _CACHE:
        _CACHE[key] = Builder(cfg).build()
    nc = _CACHE[key]
    maps = prepare_inputs(cfg, inp)
    res = run_bass_kernel_spmd(nc, maps, core_ids=list(range(cfg.ncores)))
    ys = [np.asarray(r["yout"]) for r in res.results]
    yp = np.stack([y[:cfg.Lp] for y in ys], axis=0)
    ysm = np.concatenate([y[cfg.Lp:] for y in ys], axis=0)[None]
    return yp.astype(np.float32), ysm.astype(np.float32)


def kernel(**inputs):
    cfg = Cfg(Lp=4096, Ls=2048, depth=4)
    return run(cfg, inputs)
```
